# Optimizing a Trainium2 kernel written in Bass

```python
import math
import jax, jax.numpy as jnp
from jax import lax
import numpy as np

D_MODEL = 2048
BATCH = 16
SEQ = 2048
DEPTH = 4

GRID_W = 64
CTX_LEN = 256
MIX_W = D_MODEL
DIFF_W = MIX_W // 4
GQA_W = MIX_W // 2
HY_CH = MIX_W - DIFF_W - GQA_W
DIFF_V = 128
DIFF_QK = DIFF_V // 2
DIFF_HEADS = DIFF_W // DIFF_V
GQA_HD = 128
GQA_HEADS = GQA_W // GQA_HD
GQA_KV_HEADS = 2
GQA_GROUP = GQA_HEADS // GQA_KV_HEADS
HY_EMB = 33
HY_BANDS = (HY_EMB - 1) // 2
HY_FFN = 64
HY_FAST_DECAY = 0.3
HY_SLOW_DECAY = 1.5
HY_TARGET = 1e-2
HY_SHIFT = 0.0
SHORT_K = 3
D_FF = 4 * D_MODEL
N_MOD = 6
ROPE_THETA = 10000.0
Q_BLOCK = 128
EPS = 1e-6

DQ_W = DIFF_HEADS * 2 * DIFF_QK
GQ_W = GQA_HEADS * GQA_HD
HY_W = 3 * HY_CH
DK_W = DQ_W
DV_W = DIFF_HEADS * DIFF_V
GK_W = GQA_KV_HEADS * GQA_HD
GV_W = GK_W
KV_OFF = DQ_W + GQ_W + HY_W
KV_W = DK_W + DV_W + GK_W + GV_W
N_IN = KV_OFF + KV_W

kernel_name = "hymba_style_diffattn_gqa_hyena_dit"


def rmsnorm(x, g):
    xf = x.astype(jnp.float32)
    y = xf * lax.rsqrt(jnp.mean(xf * xf, axis=-1, keepdims=True) + EPS)
    return (y * g.astype(jnp.float32)).astype(x.dtype)


def modulate(x, shift, scale):
    return x * (1 + scale) + shift


def axial_rope(n_rows, head_dim):
    t_row = jnp.repeat(jnp.arange(n_rows, dtype=jnp.float32), GRID_W)
    t_col = jnp.tile(jnp.arange(GRID_W, dtype=jnp.float32), n_rows)
    d_axis = head_dim // 2
    inv = ROPE_THETA ** (-jnp.arange(0, d_axis, 2, dtype=jnp.float32) / d_axis)
    ang = jnp.concatenate([t_row[:, None] * inv, t_col[:, None] * inv], axis=-1)
    return jnp.cos(ang), jnp.sin(ang)


def apply_rope(x, cos, sin):
    xf = x.astype(jnp.float32).reshape(*x.shape[:-1], -1, 2)
    x0, x1 = xf[..., 0], xf[..., 1]
    out = jnp.stack([x0 * cos - x1 * sin, x0 * sin + x1 * cos], axis=-1)
    return out.reshape(x.shape).astype(x.dtype)


def heads(t, *dims):
    b, n = t.shape[:2]
    return jnp.moveaxis(t.reshape(b, n, *dims), 1, -2)


def merge(t):
    t = jnp.moveaxis(t, -2, 1)
    return t.reshape(t.shape[0], t.shape[1], -1)


def split_front(h):
    return jnp.split(h, [DQ_W, DQ_W + GQ_W], axis=-1)


def split_kv(h):
    return jnp.split(h, [DK_W, DK_W + DV_W, DK_W + DV_W + GK_W], axis=-1)


def sweep_query_blocks(fn, q):
    n = q.shape[-2]
    nb = n // Q_BLOCK
    qb = jnp.moveaxis(q.reshape(*q.shape[:-2], nb, Q_BLOCK, q.shape[-1]), -3, 0)
    out = jnp.moveaxis(lax.map(fn, qb), 0, -3)
    return out.reshape(*out.shape[:-3], n, out.shape[-1])


def diff_lambda(lam_params, lam_init):
    p = lam_params.astype(jnp.float32)
    return jnp.exp(jnp.sum(p[0] * p[1])) - jnp.exp(jnp.sum(p[2] * p[3])) + lam_init


def diff_attend(q, k, v, lam):
    s = jnp.einsum('bhmqd,bhmkd->bhmqk', q, k).astype(jnp.float32) * (DIFF_QK ** -0.5)
    p = jax.nn.softmax(s, axis=-1)
    a = p[:, :, 0] - lam * p[:, :, 1]
    return jnp.einsum('bhqk,bhkd->bhqd', a.astype(v.dtype), v)


def gqa_attend(q, k, v):
    s = jnp.einsum('bgrqd,bgkd->bgrqk', q, k).astype(jnp.float32) * (GQA_HD ** -0.5)
    p = jax.nn.softmax(s, axis=-1).astype(v.dtype)
    return jnp.einsum('bgrqk,bgkd->bgrqd', p, v)


def hyena_filters(n, w1, b1, w2, b2, w3, b3, wout, freq):
    f32 = jnp.float32
    t = jnp.linspace(0.0, 1.0, n, dtype=f32)[:, None]
    w = 2.0 * math.pi * jnp.arange(n, dtype=f32)[:, None] / n
    f = jnp.linspace(1e-4, HY_BANDS - 1, HY_BANDS, dtype=f32)[None, :]
    z = jnp.concatenate([t, jnp.cos(f * w), -jnp.sin(f * w)], axis=-1)
    fr = freq.astype(f32)
    h = jnp.sin(fr[0] * (z @ w1.astype(f32) + b1.astype(f32)))
    h = jnp.sin(fr[1] * (h @ w2.astype(f32) + b2.astype(f32)))
    h = jnp.sin(fr[2] * (h @ w3.astype(f32) + b3.astype(f32)))
    h = h @ wout.astype(f32)
    min_decay = math.log(HY_TARGET) / HY_SLOW_DECAY
    max_decay = math.log(HY_TARGET) / HY_FAST_DECAY
    deltas = jnp.linspace(min_decay, max_decay, HY_CH, dtype=f32)
    decay = jnp.exp(-t * jnp.abs(deltas))
    return h * (jnp.concatenate([decay, decay], axis=-1) + HY_SHIFT)


def bidir_long_conv(u, h2, bias):
    n, ch = u.shape[1], u.shape[2]
    hf, hb = h2[:, :ch], h2[:, ch:]
    h_full = jnp.concatenate([hf, jnp.zeros((1, ch), hf.dtype), hb[:0:-1]], axis=0)
    uf = u.astype(jnp.float32)
    y = jnp.fft.irfft(jnp.fft.rfft(uf, n=2 * n, axis=1) * jnp.fft.rfft(h_full, n=2 * n, axis=0)[None],
                      n=2 * n, axis=1)[:, :n]
    return (y + uf * bias.astype(jnp.float32)).astype(u.dtype)


def short_conv3(u, w, b):
    up = jnp.pad(u, ((0, 0), (1, 1), (0, 0)))
    return up[:, :-2] * w[0] + up[:, 1:-1] * w[1] + up[:, 2:] * w[2] + b


def hyena_mixer(hy, conv_w, conv_b, filt, bias):
    u = short_conv3(hy, conv_w, conv_b)
    x0, x1, v = jnp.split(u, 3, axis=-1)
    return x0 * bidir_long_conv(v * x1, filt, bias)


def sq_relu_mlp(x, w_up, w_down):
    return jnp.square(jax.nn.relu(x @ w_up)) @ w_down


def setup_inputs(seed: int = 0) -> dict:
    key = jax.random.key(seed)
    ks = jax.random.split(key, 32)

    def nrm(i, shape, scale):
        return jax.random.normal(ks[i], shape, jnp.float32) * scale

    nl = DEPTH
    return {
        "x": nrm(0, (BATCH, SEQ, D_MODEL), 1.0),
        "c": nrm(1, (BATCH, D_MODEL), 1.0),
        "ctx": nrm(2, (BATCH, CTX_LEN, D_MODEL), 1.0),
        "c_ctx": nrm(3, (D_MODEL,), 1.0),
        "w_mod": nrm(4, (nl, D_MODEL, N_MOD * D_MODEL), 0.5 * D_MODEL ** -0.5),
        "b_mod": nrm(5, (nl, N_MOD * D_MODEL), 0.02),
        "g_norm": 1.0 + nrm(6, (nl, 4, D_MODEL), 0.1),
        "w_in": nrm(7, (nl, D_MODEL, N_IN), D_MODEL ** -0.5),
        "w_out": nrm(8, (nl, MIX_W, D_MODEL), MIX_W ** -0.5),
        "diff_lam": nrm(9, (nl, 4, DIFF_QK), 0.1),
        "diff_subln": 1.0 + nrm(10, (nl, DIFF_V), 0.1),
        "gqa_q_norm": 1.0 + nrm(11, (nl, GQA_HD), 0.1),
        "gqa_k_norm": 1.0 + nrm(12, (nl, GQA_HD), 0.1),
        "gqa_out_norm": 1.0 + nrm(13, (nl, GQA_HD), 0.1),
        "hy_conv_w": nrm(14, (nl, SHORT_K, HY_W), SHORT_K ** -0.5),
        "hy_conv_b": nrm(15, (nl, HY_W), 0.02),
        "hy_w1": nrm(16, (nl, HY_EMB, HY_FFN), HY_EMB ** -0.5),
        "hy_b1": nrm(17, (nl, HY_FFN), 0.1),
        "hy_w2": nrm(18, (nl, HY_FFN, HY_FFN), HY_FFN ** -0.5),
        "hy_b2": nrm(19, (nl, HY_FFN), 0.1),
        "hy_w3": nrm(20, (nl, HY_FFN, HY_FFN), HY_FFN ** -0.5),
        "hy_b3": nrm(21, (nl, HY_FFN), 0.1),
        "hy_wout": nrm(22, (nl, HY_FFN, 2 * HY_CH), 0.1 * HY_FFN ** -0.5),
        "hy_freq": 1.0 + nrm(23, (nl, 3, HY_FFN), 0.1),
        "hy_bias": nrm(24, (nl, HY_CH), 0.5),
        "hy_out_norm": 1.0 + nrm(25, (nl, HY_CH), 0.1),
        "w_up": nrm(26, (nl, D_MODEL, D_FF), D_MODEL ** -0.5),
        "w_down": nrm(27, (nl, D_FF, D_MODEL), D_FF ** -0.5),
    }


def reference(x, c, ctx, c_ctx, w_mod, b_mod, g_norm, w_in, w_out, diff_lam, diff_subln,
              gqa_q_norm, gqa_k_norm, gqa_out_norm, hy_conv_w, hy_conv_b, hy_w1, hy_b1,
              hy_w2, hy_b2, hy_w3, hy_b3, hy_wout, hy_freq, hy_bias, hy_out_norm, w_up, w_down):
    n_lat = x.shape[1]
    n_ctx = ctx.shape[1]
    ROWS = n_lat // GRID_W
    cos_d, sin_d = axial_rope(ROWS, DIFF_QK)
    cos_g, sin_g = axial_rope(ROWS, GQA_HD)
    s_c = jax.nn.silu(c)
    s_cc = jax.nn.silu(c_ctx)
    xc = ctx
    for l in range(DEPTH):
        last = l == DEPTH - 1
        lam_init = 0.8 - 0.6 * math.exp(-0.3 * l)
        lam = diff_lambda(diff_lam[l], lam_init)
        sh_a, sc_a, gt_a, sh_m, sc_m, gt_m = jnp.split((s_c @ w_mod[l] + b_mod[l])[:, None, :], N_MOD, axis=-1)
        csh_a, csc_a, cgt_a, csh_m, csc_m, cgt_m = jnp.split(s_cc @ w_mod[l] + b_mod[l], N_MOD, axis=-1)
        filt_args = (hy_w1[l], hy_b1[l], hy_w2[l], hy_b2[l], hy_w3[l], hy_b3[l], hy_wout[l], hy_freq[l])

        xn = modulate(rmsnorm(x, g_norm[l, 0]), sh_a, sc_a)
        xcn = modulate(rmsnorm(xc, g_norm[l, 0]), csh_a, csc_a)
        h = xn @ w_in[l]
        hc = xcn @ (w_in[l][:, KV_OFF:] if last else w_in[l])
        dq, gq, hy = split_front(h[..., :KV_OFF])
        dk, dv, gk, gv = split_kv(h[..., KV_OFF:])
        cdk, cdv, cgk, cgv = split_kv(hc[..., -KV_W:])

        q_d = apply_rope(heads(dq, DIFF_HEADS, 2, DIFF_QK), cos_d, sin_d)
        k_d = apply_rope(heads(dk, DIFF_HEADS, 2, DIFF_QK), cos_d, sin_d)
        kc_d = heads(cdk, DIFF_HEADS, 2, DIFF_QK)
        vc_d = heads(cdv, DIFF_HEADS, DIFF_V)
        kd_all = jnp.concatenate([kc_d, k_d], axis=-2)
        vd_all = jnp.concatenate([vc_d, heads(dv, DIFF_HEADS, DIFF_V)], axis=-2)
        o_d = sweep_query_blocks(lambda qb: diff_attend(qb, kd_all, vd_all, lam), q_d)
        o_d = merge(rmsnorm(o_d, diff_subln[l]) * (1.0 - lam_init))

        q_g = apply_rope(rmsnorm(heads(gq, GQA_KV_HEADS, GQA_GROUP, GQA_HD), gqa_q_norm[l]), cos_g, sin_g)
        k_g = apply_rope(rmsnorm(heads(gk, GQA_KV_HEADS, GQA_HD), gqa_k_norm[l]), cos_g, sin_g)
        kc_g = rmsnorm(heads(cgk, GQA_KV_HEADS, GQA_HD), gqa_k_norm[l])
        vc_g = heads(cgv, GQA_KV_HEADS, GQA_HD)
        kg_all = jnp.concatenate([kc_g, k_g], axis=-2)
        vg_all = jnp.concatenate([vc_g, heads(gv, GQA_KV_HEADS, GQA_HD)], axis=-2)
        o_g = sweep_query_blocks(lambda qb: gqa_attend(qb, kg_all, vg_all), q_g)
        o_g = merge(rmsnorm(o_g, gqa_out_norm[l]))

        filt_lat = hyena_filters(n_lat, *filt_args)
        o_h = rmsnorm(hyena_mixer(hy, hy_conv_w[l], hy_conv_b[l], filt_lat, hy_bias[l]), hy_out_norm[l])

        mix = jnp.concatenate([o_d, o_g, o_h], axis=-1) @ w_out[l]
        x = x + gt_a * rmsnorm(mix, g_norm[l, 1])

        if not last:
            cdq, cgq, chy = split_front(hc[..., :KV_OFF])
            oc_d = diff_attend(heads(cdq, DIFF_HEADS, 2, DIFF_QK), kc_d, vc_d, lam)
            oc_d = merge(rmsnorm(oc_d, diff_subln[l]) * (1.0 - lam_init))
            oc_g = gqa_attend(rmsnorm(heads(cgq, GQA_KV_HEADS, GQA_GROUP, GQA_HD), gqa_q_norm[l]), kc_g, vc_g)
            oc_g = merge(rmsnorm(oc_g, gqa_out_norm[l]))
            filt_ctx = hyena_filters(n_ctx, *filt_args)
            oc_h = rmsnorm(hyena_mixer(chy, hy_conv_w[l], hy_conv_b[l], filt_ctx, hy_bias[l]), hy_out_norm[l])
            mix_c = jnp.concatenate([oc_d, oc_g, oc_h], axis=-1) @ w_out[l]
            xc = xc + cgt_a * rmsnorm(mix_c, g_norm[l, 1])

        xn = modulate(rmsnorm(x, g_norm[l, 2]), sh_m, sc_m)
        x = x + gt_m * rmsnorm(sq_relu_mlp(xn, w_up[l], w_down[l]), g_norm[l, 3])
        if not last:
            xcn = modulate(rmsnorm(xc, g_norm[l, 2]), csh_m, csc_m)
            xc = xc + cgt_m * rmsnorm(sq_relu_mlp(xcn, w_up[l], w_down[l]), g_norm[l, 3])
    return x
```

```python
import numpy as np
import concourse.bass as bass
import concourse.mybir as mybir
from contextlib import ExitStack

F32 = mybir.dt.float32
BF16 = mybir.dt.bfloat16
ALU = mybir.AluOpType
AF = mybir.ActivationFunctionType
AX = mybir.AxisListType
from concourse.bass_utils import run_bass_kernel_spmd


class Buf:
    __slots__ = ("name", "w", "r", "parent", "children", "excl")

    def __init__(self, name, parent=None):
        self.name = name
        self.excl = False
        self.w = None
        self.r = {}
        self.parent = parent
        self.children = {}

    def child(self, key):
        c = self.children.get(key)
        if c is None:
            c = Buf(f"{self.name}/{key}", self)
            self.children[key] = c
        return c

    def _family(self):
        out = [self]
        p = self.parent
        while p is not None:
            out.append(p)
            p = p.parent
        stack = list(self.children.values())
        while stack:
            c = stack.pop()
            out.append(c)
            stack.extend(c.children.values())
        return out

    def _desc(self):
        out = []
        stack = list(self.children.values())
        while stack:
            c = stack.pop()
            out.append(c)
            stack.extend(c.children.values())
        return out


class T:
    __slots__ = ("buf", "ap")

    def __init__(self, buf, ap):
        self.buf = buf
        self.ap = ap

    def __getitem__(self, idx):
        return T(self.buf, self.ap[idx])

    def reg(self, key):
        return T(self.buf.child(key), self.ap)

    def rearrange(self, *a, **k):
        return T(self.buf, self.ap.rearrange(*a, **k))


class Ctx:
    def __init__(self, nc):
        self.nc = nc
        self.es = ExitStack()
        self.root_es = self.es
        self.sems = []
        self.sem_is_dma = []
        self.dma_issued = []
        self.dma_keys = {}
        self.engs = {}
        for name, eng in (("pe", nc.tensor), ("act", nc.scalar), ("dve", nc.vector),
                          ("pool", nc.gpsimd), ("sp", nc.sync)):
            sid = self._new_sem(f"s_{name}", False)
            self.engs[name] = dict(eng=eng, sid=sid, count=0, waited={})
        self.n_inst = 0

    def _new_sem(self, name, is_dma):
        h = self.root_es.enter_context(self.nc.semaphore(name))
        self.sems.append(h)
        self.sem_is_dma.append(is_dma)
        self.dma_issued.append(0)
        return len(self.sems) - 1

    def close(self):
        self.es.close()

    def push_scope(self):
        self._outer = getattr(self, "_outer", [])
        self._outer.append(self.es)
        self.es = ExitStack()

    def pop_scope(self):
        self.barrier()
        self.es.close()
        self.es = self._outer.pop()

    def barrier(self):
        if getattr(self, "dead", False):
            return
        deps = {}
        for n, e in self.engs.items():
            if e["count"] > 0:
                deps[e["sid"]] = e["count"]
        for sid in range(len(self.sems)):
            if self.sem_is_dma[sid] and self.dma_issued[sid] > 0:
                deps[sid] = 16 * self.dma_issued[sid]
        for n in self.engs:
            self._wait(n, dict(deps))

    def sbuf(self, name, shape, dt):
        self._uid = getattr(self, "_uid", 0) + 1
        h = self.es.enter_context(self.nc.sbuf_tensor(f"sb{self._uid}_{name}", list(shape), dt))
        return T(Buf(name), h[:] if False else h.ap() if hasattr(h, "ap") else h[:])

    def psum(self, name, shape, dt):
        self._uid = getattr(self, "_uid", 0) + 1
        h = self.es.enter_context(self.nc.psum_tensor(f"ps{self._uid}_{name}", list(shape), dt))
        b = Buf(name)
        b.excl = True
        return T(b, h.ap() if hasattr(h, "ap") else h[:])

    def dram(self, name, shape, dt, kind="Internal"):
        h = self.nc.dram_tensor(name, list(shape), dt, kind=kind)
        return T(Buf(name), h.ap())

    def _deps(self, reads, writes):
        deps = {}
        writes = list(writes) + [t for t in reads if t.buf.excl]

        def add(ev):
            if ev is None:
                return
            sid, val = ev
            if deps.get(sid, 0) < val:
                deps[sid] = val

        for t in reads:
            for b in t.buf._family():
                add(b.w)
        for t in writes:
            for b in t.buf._family():
                add(b.w)
                for sid, val in b.r.items():
                    add((sid, val))
        return deps

    def _wait(self, ename, deps):
        e = self.engs[ename]
        for sid, val in deps.items():
            if self.sem_is_dma[sid]:
                val = max(val, 16 * self.dma_issued[sid])
            if e["waited"].get(sid, 0) < val:
                e["eng"].wait_ge(self.sems[sid], val)
                e["waited"][sid] = val

    def _record(self, ev, reads, writes):
        sid, val = ev
        writes = list(writes) + [t for t in reads if t.buf.excl]
        for t in reads:
            b = t.buf
            if b.r.get(sid, 0) < val:
                b.r[sid] = val
        for t in writes:
            b = t.buf
            b.w = ev
            b.r = {}
            for c in b._desc():
                c.w = ev
                c.r = {}

    def op(self, ename, fn, reads, writes):
        if getattr(self, "dead", False):
            return None
        self._wait(ename, self._deps(reads, writes))
        e = self.engs[ename]
        ins = fn(e["eng"])
        e["count"] += 1
        ins.then_inc(self.sems[e["sid"]], 1)
        self._record((e["sid"], e["count"]), reads, writes)
        self.n_inst += 1
        return ins

    def group(self, ename, fns, reads, writes):
        if getattr(self, "dead", False):
            return None
        self._wait(ename, self._deps(reads, writes))
        e = self.engs[ename]
        ins = None
        for fn in fns:
            ins = fn(e["eng"])
            self.n_inst += 1
        e["count"] += 1
        ins.then_inc(self.sems[e["sid"]], 1)
        self._record((e["sid"], e["count"]), reads, writes)

    def dma(self, qname, out, in_, key, throttle=None, **kw):
        if getattr(self, "dead", False):
            return None
        sid = self.dma_keys.get(key)
        if sid is None:
            sid = self._new_sem("d_" + key, True)
            self.dma_keys[key] = sid
        self._wait(qname, self._deps([in_], [out]))
        e = self.engs[qname]
        if throttle is not None and self.dma_issued[sid] >= throttle:
            v = 16 * (self.dma_issued[sid] - throttle + 1)
            if e["waited"].get(sid, 0) < v:
                e["eng"].wait_ge(self.sems[sid], v)
                e["waited"][sid] = v
        ins = e["eng"].dma_start(out=out.ap, in_=in_.ap, **kw)
        self.dma_issued[sid] += 1
        ins.then_inc(self.sems[sid], 16)
        self._record((sid, 16 * self.dma_issued[sid]), [in_], [out])
        self.n_inst += 1
        return ins

    def finish(self, outs, ename="sp"):
        deps = {}
        for t in outs:
            for b in t.buf._family():
                if b.w is not None:
                    sid, val = b.w
                    deps[sid] = max(deps.get(sid, 0), val)
        for n, e in self.engs.items():
            if e["count"] > 0:
                deps[e["sid"]] = max(deps.get(e["sid"], 0), e["count"])
        for sid in range(len(self.sems)):
            if self.sem_is_dma[sid] and self.dma_issued[sid] > 0:
                deps[sid] = 16 * self.dma_issued[sid]
        self._wait(ename, deps)

    def mm(self, out, lhsT, rhs, start=True, stop=True, **kw):
        return lambda eng: eng.matmul(out.ap, lhsT.ap, rhs.ap, start=start, stop=stop, **kw)

    def matmul_group(self, out, pairs, extra_reads=()):
        n = len(pairs)
        fns = [self.mm(out, l, r, start=(i == 0), stop=(i == n - 1)) for i, (l, r) in enumerate(pairs)]
        reads = [x for p in pairs for x in p] + list(extra_reads)
        self.group("pe", fns, reads, [out])

    def transpose(self, out, in_, ident):
        self.op("pe", lambda eng: eng.transpose(out.ap, in_.ap, ident.ap), [in_, ident], [out])

    def act(self, out, in_, func, bias=None, scale=None, accum_out=None, eng="act"):
        kw = {}
        reads = [in_]
        writes = [out]
        if bias is not None:
            if isinstance(bias, T):
                kw["bias"] = bias.ap
                reads.append(bias)
            else:
                kw["bias"] = bias
        if scale is not None:
            if isinstance(scale, T):
                kw["scale"] = scale.ap
                reads.append(scale)
            else:
                kw["scale"] = scale
        if accum_out is not None:
            kw["accum_out"] = accum_out.ap
            writes.append(accum_out)
        return self.op(eng, lambda e: e.activation(out.ap, in_.ap, func, **kw), reads, writes)

    def tt(self, out, a, b, op, eng="dve"):
        return self.op(eng, lambda e: e.tensor_tensor(out.ap, a.ap, b.ap, op), [a, b], [out])

    def ts(self, out, a, s1, s2, op0, op1=None, eng="dve"):
        reads = [a]
        v1 = s1
        v2 = s2
        if isinstance(s1, T):
            reads.append(s1)
            v1 = s1.ap
        if isinstance(s2, T):
            reads.append(s2)
            v2 = s2.ap
        if op1 is None:
            return self.op(eng, lambda e: e.tensor_scalar(out.ap, a.ap, v1, None, op0), reads, [out])
        return self.op(eng, lambda e: e.tensor_scalar(out.ap, a.ap, v1, v2, op0, op1), reads, [out])

    def stt(self, out, a, s, b, op0, op1, eng="dve"):
        reads = [a, b]
        v = s
        if isinstance(s, T):
            reads.append(s)
            v = s.ap
        return self.op(eng, lambda e: e.scalar_tensor_tensor(out.ap, a.ap, v, b.ap, op0, op1), reads, [out])

    def copy(self, out, in_, eng="dve"):
        if eng == "act":
            return self.op("act", lambda e: e.copy(out.ap, in_.ap), [in_], [out])
        return self.op(eng, lambda e: e.tensor_copy(out.ap, in_.ap), [in_], [out])

    def memset(self, out, val, eng="dve"):
        return self.op(eng, lambda e: e.memset(out.ap, val), [], [out])

    def recip(self, out, in_):
        return self.op("dve", lambda e: e.reciprocal(out.ap, in_.ap), [in_], [out])

    def reduce(self, out, in_, op=None, eng="dve"):
        op = op or ALU.add
        return self.op(eng, lambda e: e.tensor_reduce(out.ap, in_.ap, AX.X, op), [in_], [out])
import math

D = 2048
KC = D // 128
CTX = 256
EPS = 1e-6
N_IN = 4608
MAGIC = 12582912.0
PI = math.pi


class Rot:
    def __init__(self, cx, name, n, shape, dt):
        self.slots = [cx.sbuf(f"{name}{i}", shape, dt) for i in range(n)]
        self.keys = [f"{name}{i}" for i in range(n)]
        self.i = 0

    def next(self):
        s, k = self.slots[self.i], self.keys[self.i]
        self.i = (self.i + 1) % len(self.slots)
        return s, k


class PRot:
    def __init__(self, banks):
        self.b = banks
        self.i = 0

    def next(self):
        s = self.b[self.i]
        self.i = (self.i + 1) % len(self.b)
        return s


def build_program(nc, cfg):
    L, SEQ, NSEQ, DFF = cfg["L"], cfg["SEQ"], cfg["NSEQ"], cfg["DFF"]
    LT = CTX + SEQ
    NR = NSEQ + 1
    HC = DFF // 128
    NUG = DFF // 512
    HH = HC // 2
    cx = Ctx(nc)
    blocks = [(0, CTX)] + [(CTX + i * 512, 512) for i in range(SEQ // 512)]
    NT = LT // 128

    def din(name, shape, dt=F32):
        return cx.dram(name, shape, dt, kind="ExternalInput")

    _cxdram = cx.dram

    def dscr(name, shape, dt, kind="Internal"):
        if cfg.get("dbgout") and kind == "Internal":
            kind = "ExternalOutput"
        return _cxdram(name, shape, dt, kind=kind)

    x_in = din("x", [NSEQ, SEQ, D])
    ctx_in = din("ctx", [NSEQ, CTX, D])
    out_t = cx.dram("out", [NSEQ, SEQ, D], F32, kind="ExternalOutput")
    scT_in = din("scT", [128, KC, NR])
    w_mod = din("w_mod", [L, D, 6 * D])
    bmodT_in = din("bmodT", [128, L, 96])
    gnT_in = din("gnT", [128, L, 4, KC])
    w_in = din("w_in", [L, D, N_IN])
    w_out = din("w_out", [L, D, D])
    w_up = din("w_up", [L, D, DFF])
    w_down = din("w_down", [L, DFF, D])
    vec128_in = din("vec128", [128, L, 4])
    lamp_in = din("lamp", [128, L, 256])
    convw_in = din("convw", [128, L, 12, 3])
    convb_in = din("convb", [128, L, 12])
    hyon_in = din("hyon", [128, L, 4])
    hybias_in = din("hybias", [1, L, 512])
    hw1_in = din("hw1", [L, 33, 64])
    hw2_in = din("hw2", [L, 64, 64])
    hw3_in = din("hw3", [L, 64, 64])
    hwout_in = din("hwout", [L, 64, 1024])
    hbT_in = din("hbT", [64, L, 3])
    hfT_in = din("hfT", [64, L, 3])
    ident_in = din("ident", [128, 128])
    rswap_in = din("rswap", [128, 128])
    rope_in = din("rope", [4, 128, SEQ])
    hyc = {}
    for tag, n in (("lat", SEQ), ("ctx", CTX)):
        nfc = n // 128
        hyc[tag] = dict(
            n=n, nfc=nfc,
            zT=din(f"zT_{tag}", [33, n]),
            decay=din(f"decay_{tag}", [nfc, 128, 512]),
            cmF32=din(f"cmF_{tag}", [nfc, 128, nfc, 128]),
            smF32=din(f"smF_{tag}", [nfc, 128, nfc, 128]),
            cmI32=din(f"cmI_{tag}", [nfc, 128, n]),
            smI32=din(f"smI_{tag}", [nfc, 128, n]),
            cmF=dscr(f"cmFb_{tag}", [nfc, 128, nfc, 128], BF16),
            smF=dscr(f"smFb_{tag}", [nfc, 128, nfc, 128], BF16),
            cmI=dscr(f"cmIb_{tag}", [nfc, 128, n], BF16),
            smI=dscr(f"smIb_{tag}", [nfc, 128, n], BF16),
            hre=dscr(f"hre_{tag}", [nfc, 128, 512], F32),
            him=dscr(f"him_{tag}", [nfc, 128, 512], F32),
        )

    XT = dscr("XT", [NSEQ, KC, 128, LT], F32)
    QT = dscr("QT", [12, 128, LT], BF16)
    KT = dscr("KT", [6, 128, LT], BF16)
    VV = dscr("VV", [NT, 128, 768], BF16)
    HY = dscr("HY", [12, 128, LT], F32)
    OT = dscr("OT", [KC, 128, LT], BF16)
    WIN = [dscr(f"WIN{l}", [18, 128, KC, 256], BF16) for l in range(L)]
    WOUT = [dscr(f"WOUT{l}", [4, 128, KC, 512], BF16) for l in range(L)]
    WUP = [dscr(f"WUP{l}", [NUG, 128, KC, 512], BF16) for l in range(L)]
    WDN = [dscr(f"WDN{l}", [2, KC, 128, HH, 128], BF16) for l in range(L)]

    ident_f = cx.sbuf("ident_f", [128, 128], F32)
    ident_b = cx.sbuf("ident_b", [128, 128], BF16)
    ones_b = cx.sbuf("ones_b", [128, 128], BF16)
    rswap_b = cx.sbuf("rswap_b", [128, 128], BF16)
    eps_t = cx.sbuf("eps_t", [128, 1], F32)
    modT = cx.sbuf("modT", [128, L, 96, NR], F32)
    bmodT = cx.sbuf("bmodT", [128, L, 96], F32)
    gnT = cx.sbuf("gnT", [128, L, 4, KC], F32)
    vec128 = cx.sbuf("vec128", [128, L, 4], F32)
    convw = cx.sbuf("convw", [128, L, 12, 3], F32)
    convb = cx.sbuf("convb", [128, L, 12], F32)
    hyon = cx.sbuf("hyon", [128, L, 4], F32)
    lam_t = cx.sbuf("lam_t", [128, L, 2], F32)
    psb = [cx.psum(f"psb{i}", [128, 512], F32) for i in range(8)]

    cx.dma("sp", ident_f, ident_in, "cld")
    cx.copy(ident_b, ident_f)
    tmpf = cx.sbuf("tmpf", [128, 128], F32)
    cx.dma("sp", tmpf, rswap_in, "cld")
    cx.copy(rswap_b, tmpf)
    cx.memset(ones_b, 1.0)
    cx.memset(eps_t, EPS)
    cx.dma("sp", bmodT, bmodT_in, "cld")
    cx.dma("sp", gnT, gnT_in, "cld")
    cx.dma("sp", vec128, vec128_in, "cld")
    cx.dma("sp", convw, convw_in, "cld")
    cx.dma("sp", convb, convb_in, "cld")
    cx.dma("sp", hyon, hyon_in, "cld")

    for tag in ("lat", "ctx"):
        h = hyc[tag]
        for a, b in (("cmF32", "cmF"), ("smF32", "smF"), ("cmI32", "cmI"), ("smI32", "smI")):
            for fc in range(h["nfc"]):
                cx.dma("pool", h[b][fc], h[a][fc], "cast_tab", throttle=2)

    def cast_layer(l):
        key = f"cast{l}"
        for g in range(18):
            cx.dma("pool", WIN[l][g].reg(g), w_in[l][:, g * 256:(g + 1) * 256].rearrange("(kc p) n -> p kc n", p=128), key, throttle=2)
        for g in range(4):
            cx.dma("pool", WOUT[l][g].reg(g), w_out[l][:, g * 512:(g + 1) * 512].rearrange("(kc p) n -> p kc n", p=128), key, throttle=2)
        for g in range(NUG):
            cx.dma("pool", WUP[l][g].reg(g), w_up[l][:, g * 512:(g + 1) * 512].rearrange("(kc p) n -> p kc n", p=128), key, throttle=2)
        for hf in range(2):
            for oc in range(KC):
                cx.dma("pool", WDN[l][hf][oc].reg((hf, oc)),
                       w_down[l][hf * HH * 128:(hf + 1) * HH * 128, oc * 128:(oc + 1) * 128].rearrange("(kc p) n -> p kc n", p=128), key, throttle=2)

    def stage(k):
        if cfg.get("stop") == k:
            cx.dead = True

    cast_layer(0)
    stage(0)

    cx.push_scope()
    scT = cx.sbuf("scT", [128, KC, NR], F32)
    cx.dma("sp", scT, scT_in, "pld")
    cx.act(scT, scT, AF.Silu)
    wm = Rot(cx, "wm", 2, [128, KC, 256], F32)
    pr = PRot(psb[0:4])
    for l in range(L):
        for g in range(6 * D // 256):
            wt, key = wm.next()
            cx.dma("sp", wt, w_mod[l][:, g * 256:(g + 1) * 256].rearrange("(kc p) n -> p kc n", p=128), key, throttle=2)
            for j in range(2):
                oc = g * 2 + j
                ps = pr.next()
                cx.matmul_group(ps[:, 0:NR], [(wt[:, kc, j * 128:(j + 1) * 128], scT[:, kc, :]) for kc in range(KC)])
                cx.ts(modT[:, l, oc, :], ps[:, 0:NR], bmodT[:, l, oc:oc + 1], None, ALU.add)
    lamp = cx.sbuf("lamp", [128, L, 256], F32)
    cx.dma("sp", lamp, lamp_in, "pld")
    lwork = cx.sbuf("lwork", [128, 64], F32)
    lsum = cx.sbuf("lsum", [128, 2], F32)
    for l in range(L):
        lam_init = 0.8 - 0.6 * math.exp(-0.3 * l)
        for i in range(2):
            cx.tt(lwork, lamp[:, l, (2 * i) * 64:(2 * i + 1) * 64], lamp[:, l, (2 * i + 1) * 64:(2 * i + 2) * 64], ALU.mult)
            cx.reduce(lsum[:, i:i + 1], lwork)
        cx.act(lsum, lsum, AF.Exp)
        cx.tt(lam_t[:, l, 0:1], lsum[:, 1:2], lsum[:, 0:1], ALU.subtract)
        cx.ts(lam_t[:, l, 0:1], lam_t[:, l, 0:1], -lam_init, None, ALU.add)
        cx.ts(lam_t[:, l, 1:2], vec128[:, l, 0:1], 1.0 - lam_init, None, ALU.mult)
    cx.pop_scope()
    stage(1)

    cx.push_scope()
    xl = Rot(cx, "xl", 2, [128, D], F32)
    xs = Rot(cx, "xs", 2, [128, KC, 128], F32)
    pr = PRot(psb[0:8])
    for b in range(NSEQ):
        for tt in range(NT):
            xt, key = xl.next()
            src = ctx_in[b][tt * 128:(tt + 1) * 128, :] if tt < 2 else x_in[b][(tt - 2) * 128:(tt - 1) * 128, :]
            cx.dma("sp", xt, src, key)
            st, skey = xs.next()
            for q in range(KC // 4):
                ps = pr.next()
                fns = [cx.mm(ps[:, j * 128:(j + 1) * 128], xt[:, (q * 4 + j) * 128:(q * 4 + j + 1) * 128], ident_f, True, True)
                       for j in range(4)]
                cx.group("pe", fns, [xt, ident_f], [ps])
                cx.copy(st[:, q * 4:(q + 1) * 4, :], ps.rearrange("p (j t) -> p j t", j=4), eng=("act" if q % 2 else "dve"))
            cx.dma("pool", XT[b].rearrange("c p t -> p c t")[:, :, tt * 128:(tt + 1) * 128].reg(tt), st, skey)
    cx.pop_scope()
    stage(2)

    def rstd_from(ps_view, n, inv_dim, out_t, work):
        cx.act(work[:, 0:n], ps_view, AF.Ln, bias=eps_t, scale=inv_dim)
        cx.act(out_t[:, 0:n], work[:, 0:n], AF.Exp, scale=-0.5)

    def mod_vec(l, which, r):
        return modT[:, l, which * KC:(which + 1) * KC, r]

    def phase_F(l, tags):
        cx.push_scope()
        w1 = cx.sbuf("f_w1", [64, 64], F32)
        cx.memset(w1, 0.0)
        w2 = cx.sbuf("f_w2", [64, 64], F32)
        w3 = cx.sbuf("f_w3", [64, 64], F32)
        wo = cx.sbuf("f_wo", [64, 1024], F32)
        hb = cx.sbuf("f_hb", [64, L, 3], F32)
        hf = cx.sbuf("f_hf", [64, L, 3], F32)
        fb = cx.sbuf("f_fb", [64, 3], F32)
        brow = cx.sbuf("f_brow", [1, 512], F32)
        cx.dma("sp", w1[0:33, :], hw1_in[l], "fld")
        cx.dma("sp", w2, hw2_in[l], "fld")
        cx.dma("sp", w3, hw3_in[l], "fld")
        cx.dma("sp", wo, hwout_in[l], "fld")
        cx.dma("sp", hb, hbT_in, "fld")
        cx.dma("sp", hf, hfT_in, "fld")
        cx.dma("sp", brow, hybias_in[:, l, :], "fld")
        cx.tt(fb, hb[:, l, :], hf[:, l, :], ALU.mult)
        pr = PRot(psb[0:4])
        pr2 = PRot(psb[4:8])
        for tag in tags:
            h = hyc[tag]
            n, nfc = h["n"], h["nfc"]
            zT = cx.sbuf(f"f_zT{tag}", [64, n], F32)
            cx.memset(zT, 0.0)
            cx.dma("sp", zT[0:33, :], h["zT"], "fld")
            h3 = cx.sbuf(f"f_h3{tag}", [64, n], F32)
            a1 = cx.sbuf(f"f_a1{tag}", [64, 512], F32)
            a2 = cx.sbuf(f"f_a2{tag}", [64, 512], F32)
            hcur = cx.sbuf(f"f_hc{tag}", [64, 2, 512], F32)
            nb = min(512, n)
            for pb in range(n // nb):
                src = zT[:, pb * nb:(pb + 1) * nb]
                for li, w in enumerate((w1, w2, w3)):
                    ps = pr.next()
                    cx.matmul_group(ps[0:64, 0:nb], [(w, src)])
                    cx.ts(a1[:, 0:nb], ps[0:64, 0:nb], hf[:, l, li:li + 1], fb[:, li:li + 1], ALU.mult, ALU.add)
                    cx.ts(a2[:, 0:nb], a1[:, 0:nb], 1.0 / (2 * PI), MAGIC, ALU.mult, ALU.add)
                    cx.ts(a2[:, 0:nb], a2[:, 0:nb], MAGIC, -2 * PI, ALU.subtract, ALU.mult)
                    cx.tt(a2[:, 0:nb], a2[:, 0:nb], a1[:, 0:nb], ALU.add)
                    cx.ts(a2[:, 0:nb], a2[:, 0:nb], PI, -PI, ALU.min, ALU.max)
                    dst = h3[:, pb * nb:(pb + 1) * nb] if li == 2 else hcur[:, li, 0:nb]
                    cx.act(dst, a2[:, 0:nb], AF.Sin)
                    src = dst
            hsT = cx.sbuf(f"f_hs{tag}", [128, nfc, 512], BF16)
            hdT = cx.sbuf(f"f_hd{tag}", [128, nfc, 512], BF16)
            dec = Rot(cx, f"f_dec{tag}", 2, [128, 512], F32)
            hfw = cx.sbuf(f"f_hfw{tag}", [128, 512], F32)
            hbw = cx.sbuf(f"f_hbw{tag}", [128, 512], F32)
            for pc in range(nfc):
                dt_, dk = dec.next()
                cx.dma("sp", dt_, h["decay"][pc], dk)
                p0 = pr.next()
                p1 = pr.next()
                cx.matmul_group(p0, [(h3[:, pc * 128:(pc + 1) * 128], wo[:, 0:512])])
                cx.matmul_group(p1, [(h3[:, pc * 128:(pc + 1) * 128], wo[:, 512:1024])])
                cx.tt(hfw, p0, dt_, ALU.mult)
                cx.tt(hbw, p1, dt_, ALU.mult)
                if pc == 0:
                    cx.memset(hbw[0:1, :], 0.0)
                    cx.tt(hfw[0:1, :], hfw[0:1, :], brow, ALU.add)
                cx.tt(hsT[:, pc, :], hfw, hbw, ALU.add)
                cx.tt(hdT[:, pc, :], hbw, hfw, ALU.subtract)
            tabs = Rot(cx, f"f_tab{tag}", 4, [128, nfc, 128], BF16)
            stg = Rot(cx, f"f_stg{tag}", 2, [128, 512], F32)
            for fc in range(nfc):
                for tname, src_t, dst in (("cmF", hsT, h["hre"]), ("smF", hdT, h["him"])):
                    tb, tk = tabs.next()
                    cx.dma("sp", tb, h[tname][fc], tk)
                    ps = pr2.next()
                    cx.matmul_group(ps, [(tb[:, dc, :], src_t[:, dc, :]) for dc in range(nfc)])
                    s_, sk = stg.next()
                    cx.act(s_, ps, AF.Identity, scale=1.0 / n)
                    cx.dma("pool", dst[fc].reg(fc), s_, sk)
        cx.pop_scope()

    def phase_A1(l, b, last):
        cx.push_scope()
        xnT = cx.sbuf("xnT", [128, KC, LT], BF16)
        rope = cx.sbuf("rope", [128, 4, SEQ], F32)
        for i in range(4):
            cx.dma("sp", rope[:, i, :], rope_in[i], "ropeld")
        stage(31)
        xin = Rot(cx, "xin", 4, [128, 512], F32)
        sqb = Rot(cx, "sqb", 2, [128, 512], BF16)
        uf = Rot(cx, "uf", 2, [128, 512], F32)
        rstd = cx.sbuf("rstd", [128, 512], F32)
        wk = cx.sbuf("wk", [128, 512], F32)
        gp = cx.sbuf("gp", [128, KC], F32)
        ssq = psb[7]
        for (s, n) in blocks:
            r = NSEQ if s == 0 else b
            cx.ts(gp, mod_vec(l, 1, r), 1.0, None, ALU.add)
            cx.tt(gp, gp, gnT[:, l, 0, :], ALU.mult)
            for c in range(KC):
                xt, key = xin.next()
                cx.dma("sp", xt[:, 0:n], XT[b][c][:, s:s + n], key)
                sq, _ = sqb.next()
                cx.act(sq[:, 0:n], xt[:, 0:n], AF.Square)
                cx.group("pe", [cx.mm(ssq[:, 0:n], ones_b, sq[:, 0:n], c == 0, c == KC - 1)], [ones_b, sq], [ssq])
            rstd_from(ssq[:, 0:n], n, 1.0 / D, rstd, wk)
            for c in range(KC):
                xt, key = xin.next()
                cx.dma("sp", xt[:, 0:n], XT[b][c][:, s:s + n], key)
                u, _ = uf.next()
                cx.tt(u[:, 0:n], xt[:, 0:n], rstd[:, 0:n], ALU.mult)
                cx.act(xnT[:, c, s:s + n], u[:, 0:n], AF.Identity, bias=mod_vec(l, 0, r)[:, c:c + 1], scale=gp[:, c:c + 1])
        stage(32)
        wg = Rot(cx, "wg", 3, [128, KC, 256], BF16)
        acc = PRot(psb[0:3])
        rqp = PRot(psb[3:5])
        ssp = PRot(psb[5:7])
        stb = Rot(cx, "stb", 3, [128, 512], BF16)
        stf = Rot(cx, "stf", 2, [128, 512], F32)
        qn = Rot(cx, "qn", 2, [128, 512], F32)
        qb = Rot(cx, "qb", 2, [128, 512], BF16)
        t1 = Rot(cx, "t1", 2, [128, 512], F32)
        t2 = Rot(cx, "t2", 2, [128, 512], F32)
        sd = Rot(cx, "sd", 2, [128, 512], F32)
        rs = Rot(cx, "rs", 2, [128, 512], F32)
        for g in range(18):
            stage(40 + g)
            wt, wkey = wg.next()
            cx.dma("sp", wt, WIN[l][g], wkey)
            if g in (14, 15, 17):
                col0 = {14: 0, 15: 256, 17: 512}[g]
                for tt in range(NT):
                    ps = acc.next()
                    cx.matmul_group(ps[:, 0:256], [(xnT[:, kc, tt * 128:(tt + 1) * 128], wt[:, kc, :]) for kc in range(KC)])
                    st, sk = stb.next()
                    cx.copy(st[:, 0:256], ps[:, 0:256], eng="act")
                    cx.dma("pool", VV[tt][:, col0:col0 + 256].reg((tt, col0)), st[:, 0:256], sk)
                continue
            for j in range(2):
                ch = (2 * g + j) if g < 14 else (32 + j)
                if ch < 4:
                    kind, dst, gi, ti = "rope", QT[ch], None, 2
                elif ch < 12:
                    kind, dst, gi, ti = "norm", QT[ch], 1, 0
                elif ch < 24:
                    kind, dst, gi, ti = "hy", HY[ch - 12], None, None
                elif ch < 28:
                    kind, dst, gi, ti = "rope", KT[ch - 24], None, 2
                else:
                    kind, dst, gi, ti = "norm", KT[4 + ch - 32], 2, 0
                for (s, n) in blocks:
                    if last and s == 0 and ch < 24:
                        continue
                    ps = acc.next()
                    cx.matmul_group(ps[:, 0:n], [(wt[:, kc, j * 128:(j + 1) * 128], xnT[:, kc, s:s + n]) for kc in range(KC)])
                    dreg = dst[:, s:s + n].reg((ch, s))
                    if kind == "hy":
                        st, sk = stf.next()
                        cx.copy(st[:, 0:n], ps[:, 0:n], eng="act")
                        cx.dma("pool", dreg, st[:, 0:n], sk)
                        continue
                    st, sk = stb.next()
                    src = ps
                    if kind == "norm":
                        sq, _ = sqb.next()
                        cx.act(sq[:, 0:n], ps[:, 0:n], AF.Square)
                        p2 = ssp.next()
                        cx.matmul_group(p2[:, 0:n], [(ones_b, sq[:, 0:n])])
                        sdt, _ = sd.next()
                        rst, _ = rs.next()
                        rstd_from(p2[:, 0:n], n, 1.0 / 128, rst, sdt)
                        q_, _ = qn.next()
                        if s == 0:
                            cx.stt(st[:, 0:n], ps[:, 0:n], vec128[:, l, gi:gi + 1], rst[:, 0:n], ALU.mult, ALU.mult)
                        else:
                            cx.stt(q_[:, 0:n], ps[:, 0:n], vec128[:, l, gi:gi + 1], rst[:, 0:n], ALU.mult, ALU.mult)
                            src = q_
                    elif s == 0 or cfg.get("dbg") == 1:
                        cx.copy(st[:, 0:n], ps[:, 0:n], eng="act")
                    if s != 0 and cfg.get("dbg") != 1:
                        qb_, _ = qb.next()
                        cx.copy(qb_[:, 0:n], src[:, 0:n], eng="act")
                        p3 = rqp.next()
                        cx.matmul_group(p3[:, 0:n], [(rswap_b, qb_[:, 0:n])])
                        a_, _ = t1.next()
                        b_, _ = t2.next()
                        p0 = s - CTX
                        cx.tt(a_[:, 0:n], src[:, 0:n], rope[:, ti, p0:p0 + n], ALU.mult)
                        cx.tt(b_[:, 0:n], p3[:, 0:n], rope[:, ti + 1, p0:p0 + n], ALU.mult)
                        cx.tt(st[:, 0:n], a_[:, 0:n], b_[:, 0:n], ALU.add)
                    cx.dma("pool", dreg, st[:, 0:n], sk)
        cx.pop_scope()

    def phase_A2(l, b, last):
        cx.push_scope()
        LOOK = 3
        kt = cx.sbuf("kt", [128, 6, LT], BF16)
        vv = cx.sbuf("vv", [128, NT, 768], BF16)
        cx.dma("sp", kt, KT.rearrange("c p t -> p c t"), "a_kt")
        cx.dma("sp", vv, VV.rearrange("t p n -> p t n"), "a_vv")
        qs = Rot(cx, "qs", 2, [128, LT], BF16)
        pt = Rot(cx, "pt", 8, [128, 512], BF16)
        sb = PRot(psb[0:4])
        sets = [(psb[4], psb[5]), (psb[6], psb[7])]
        ow = Rot(cx, "aw_o", 4, [128, 512], F32)
        rw = Rot(cx, "aw_r", 2, [128, 512], F32)
        lnw = Rot(cx, "aw_l", 2, [128, 512], F32)
        rsw = Rot(cx, "aw_s", 2, [128, 512], F32)
        sqb = Rot(cx, "asq", 2, [128, 512], BF16)
        stb = Rot(cx, "ast", 3, [128, 512], BF16)
        pending = []
        deferred = []
        state = dict(job=0)

        def tick():
            for d_ in deferred:
                d_[0] -= 1
            while deferred and deferred[0][0] <= 0:
                deferred.pop(0)[1]()

        def make_pv(o_ps, r_ps, kc, nkc, p_, vcol, n):
            def f():
                cx.group("pe", [cx.mm(o_ps[:, 0:n], vv[:, kc, vcol:vcol + 128], p_[:, 0:n], kc == 0, kc == nkc - 1),
                                cx.mm(r_ps[:, 0:n], ones_b, p_[:, 0:n], kc == 0, kc == nkc - 1)],
                         [vv, ones_b, p_], [o_ps, r_ps])
            return f

        def make_head_epi(hd, s, n, o, sq, gain):
            def f():
                ps = sb.next()
                cx.matmul_group(ps[:, 0:n], [(ones_b, sq[:, 0:n])])
                ln_, _ = lnw.next()
                rs_, _ = rsw.next()
                rstd_from(ps[:, 0:n], n, 1.0 / 128, rs_, ln_)
                st, sk = stb.next()
                cx.stt(st[:, 0:n], o[:, 0:n], gain, rs_[:, 0:n], ALU.mult, ALU.mult)
                cx.dma("pool", OT[hd][:, s:s + n].reg((hd, s)), st[:, 0:n], sk)
            return f

        def make_job_epi(hd, s, n, o_ps, r_ps, parts, is_last_map, is_diff):
            def f():
                rr, _ = rw.next()
                cx.recip(rr[:, 0:n], r_ps[:, 0:n])
                om, _ = ow.next()
                cx.tt(om[:, 0:n], o_ps[:, 0:n], rr[:, 0:n], ALU.mult)
                parts.append(om)
                if not is_last_map:
                    return
                o = parts[0]
                if is_diff:
                    cx.stt(o[:, 0:n], parts[1][:, 0:n], lam_t[:, l, 0:1], o[:, 0:n], ALU.mult, ALU.add)
                    gain = lam_t[:, l, 1:2]
                else:
                    gain = vec128[:, l, 3:4]
                sq, _ = sqb.next()
                cx.tt(sq[:, 0:n], o[:, 0:n], o[:, 0:n], ALU.mult)
                deferred.append([LOOK, make_head_epi(hd, s, n, o, sq, gain)])
            return f

        qcur, qk = qs.next()
        cx.dma("sp", qcur, QT[0], qk)
        for hd in range(12):
            is_diff = hd < 4
            q_ = qcur
            if hd + 1 < 12:
                qcur, qk = qs.next()
                cx.dma("sp", qcur, QT[hd + 1], qk)
            for (s, n) in blocks:
                if s == 0 and last:
                    continue
                nkc = 2 if s == 0 else NT
                maps = (0, 1) if is_diff else (0,)
                parts = []
                for m in maps:
                    o_ps, r_ps = sets[state["job"] % 2]
                    state["job"] += 1
                    if is_diff:
                        kl = lambda kc, m=m, hd=hd: kt[64 * m:64 * m + 64, hd, kc * 128:(kc + 1) * 128]
                        qv = q_[64 * m:64 * m + 64, s:s + n]
                        vcol = hd * 128
                        scale = 0.125
                    else:
                        g = (hd - 4) // 4
                        kl = lambda kc, g=g: kt[:, 4 + g, kc * 128:(kc + 1) * 128]
                        qv = q_[:, s:s + n]
                        vcol = 512 + g * 128
                        scale = 128 ** -0.5
                    for kc in range(nkc):
                        sp_ = sb.next()
                        cx.matmul_group(sp_[:, 0:n], [(kl(kc), qv)])
                        p_, _ = pt.next()
                        cx.act(p_[:, 0:n], sp_[:, 0:n], AF.Exp, scale=scale)
                        pending.append(make_pv(o_ps, r_ps, kc, nkc, p_, vcol, n))
                        while len(pending) > LOOK:
                            pending.pop(0)()
                        tick()
                    pending.append(make_job_epi(hd, s, n, o_ps, r_ps, parts, m == maps[-1], is_diff))
        while pending:
            pending.pop(0)()
        while deferred:
            deferred.pop(0)[1]()
        cx.pop_scope()

    def phase_H(l, b, tags):
        for tag in tags:
            cx.push_scope()
            h = hyc[tag]
            n, nfc = h["n"], h["nfc"]
            s0 = 0 if tag == "ctx" else CTX
            nb = min(512, n)
            x0k = cx.sbuf("h_x0", [128, 4, n], F32)
            hyb = Rot(cx, "h_hy", 2, [128, n], F32)
            ub = Rot(cx, "h_ub", 2, [128, n], F32)
            vx = cx.sbuf("h_vx", [128, n], BF16)
            vxT = cx.sbuf("h_vxT", [128, nfc, 512], BF16)
            Y = cx.sbuf("h_Y", [128, 2 * nfc, 512], BF16)
            tp = PRot(psb[5:8])
            for j in range(4):
                outs = []
                for role in range(3):
                    ch = role * 4 + j
                    hy_, hk = hyb.next()
                    cx.dma("sp", hy_, HY[ch][:, s0:s0 + n], hk)
                    if role == 0:
                        u = x0k[:, j, :]
                    else:
                        u, _ = ub.next()
                    cx.act(u, hy_, AF.Identity, bias=convb[:, l, ch:ch + 1], scale=convw[:, l, ch, 1:2])
                    cx.stt(u[:, 1:n], hy_[:, 0:n - 1], convw[:, l, ch, 0:1], u[:, 1:n], ALU.mult, ALU.add)
                    cx.stt(u[:, 0:n - 1], hy_[:, 1:n], convw[:, l, ch, 2:3], u[:, 0:n - 1], ALU.mult, ALU.add)
                    outs.append(u)
                cx.tt(vx, outs[2], outs[1], ALU.mult)
                for q in range(nfc // 4 if nfc >= 4 else 1):
                    nq = min(4, nfc)
                    ps = tp.next()
                    fns = [cx.mm(ps[:, i * 128:(i + 1) * 128], vx[:, (q * 4 + i) * 128:(q * 4 + i + 1) * 128], ident_b, True, True)
                           for i in range(nq)]
                    cx.group("pe", fns, [vx, ident_b], [ps])
                    cx.copy(vxT[:, q * 4:q * 4 + nq, j * 128:(j + 1) * 128], ps[:, 0:nq * 128].rearrange("p (i t) -> p i t", i=nq),
                            eng=("act" if q % 2 else "dve"))
            tabs = Rot(cx, "h_tab", 4, [128, nfc, 128], BF16)
            hsp = Rot(cx, "h_hsp", 4, [128, 512], F32)
            uw = [Rot(cx, f"h_uw{i}", 2, [128, 512], F32) for i in range(4)]
            pr = PRot(psb[0:4])
            for fc in range(nfc):
                tc_, tk = tabs.next()
                cx.dma("sp", tc_, h["cmF"][fc], tk)
                ts_, tk2 = tabs.next()
                cx.dma("sp", ts_, h["smF"][fc], tk2)
                hre, k1 = hsp.next()
                cx.dma("sp", hre, h["hre"][fc], k1)
                him, k2 = hsp.next()
                cx.dma("sp", him, h["him"][fc], k2)
                pu = pr.next()
                pa = pr.next()
                cx.matmul_group(pu, [(tc_[:, dc, :], vxT[:, dc, :]) for dc in range(nfc)])
                cx.matmul_group(pa, [(ts_[:, dc, :], vxT[:, dc, :]) for dc in range(nfc)])
                U, _ = uw[0].next()
                A, _ = uw[1].next()
                cx.copy(U, pu, eng="act")
                cx.copy(A, pa, eng="act")
                m1, _ = uw[2].next()
                m2, _ = uw[3].next()
                cx.tt(m1, U, hre, ALU.mult)
                cx.tt(m2, A, him, ALU.mult)
                cx.tt(Y[:, fc, :], m1, m2, ALU.add)
                m1, _ = uw[2].next()
                m2, _ = uw[3].next()
                cx.tt(m1, A, hre, ALU.mult)
                cx.tt(m2, U, him, ALU.mult)
                cx.tt(Y[:, nfc + fc, :], m1, m2, ALU.subtract)
            itab = Rot(cx, "h_itab", 4, [128, n], BF16)
            ntb = n // nb
            for j in range(4):
                for fc in range(nfc):
                    tc_, tk = itab.next()
                    cx.dma("sp", tc_, h["cmI"][fc], tk)
                    ts_, tk2 = itab.next()
                    cx.dma("sp", ts_, h["smI"][fc], tk2)
                    for tb in range(ntb):
                        ps = psb[tb]
                        fns = [cx.mm(ps[:, 0:nb], Y[:, fc, j * 128:(j + 1) * 128], tc_[:, tb * nb:(tb + 1) * nb], fc == 0, False),
                               cx.mm(ps[:, 0:nb], Y[:, nfc + fc, j * 128:(j + 1) * 128], ts_[:, tb * nb:(tb + 1) * nb], False, fc == nfc - 1)]
                        cx.group("pe", fns, [Y, tc_, ts_], [ps])
                for tb in range(ntb):
                    cx.tt(x0k[:, j, tb * nb:(tb + 1) * nb], psb[tb][:, 0:nb], x0k[:, j, tb * nb:(tb + 1) * nb], ALU.mult)
            sqb = Rot(cx, "h_sq", 2, [128, 512], BF16)
            stb = Rot(cx, "h_st", 2, [128, 512], BF16)
            sdt = cx.sbuf("h_sd", [128, 512], F32)
            rst = cx.sbuf("h_rs", [128, 512], F32)
            for tb in range(ntb):
                for j in range(4):
                    sq, _ = sqb.next()
                    cx.act(sq[:, 0:nb], x0k[:, j, tb * nb:(tb + 1) * nb], AF.Square)
                    cx.group("pe", [cx.mm(psb[4][:, 0:nb], ones_b, sq[:, 0:nb], j == 0, j == 3)], [ones_b, sq], [psb[4]])
                rstd_from(psb[4][:, 0:nb], nb, 1.0 / 512, rst, sdt)
                for j in range(4):
                    st, sk = stb.next()
                    cx.stt(st[:, 0:nb], x0k[:, j, tb * nb:(tb + 1) * nb], hyon[:, l, j:j + 1], rst[:, 0:nb], ALU.mult, ALU.mult)
                    cx.dma("pool", OT[12 + j][:, s0 + tb * nb:s0 + (tb + 1) * nb].reg((12 + j, s0 + tb * nb)), st[:, 0:nb], sk)
            cx.pop_scope()

    def phase_CD(l, b, last):
        cx.push_scope()
        xT = cx.sbuf("c_xT", [128, KC, 512], F32)
        yT = cx.sbuf("c_yT", [128, KC, 512], F32)
        ab = cx.sbuf("c_ab", [128, KC, 512], BF16)
        hT = cx.sbuf("c_hT", [128, HH, 512], BF16)
        wb = Rot(cx, "c_wb", 3, [128, KC * 512], BF16)
        sqb = Rot(cx, "c_sq", 2, [128, 512], BF16)
        tw = Rot(cx, "c_tw", 2, [128, 512], F32)
        rl = Rot(cx, "c_rl", 2, [128, 512], F32)
        rstd = cx.sbuf("c_rstd", [128, 512], F32)
        wk = cx.sbuf("c_wk", [128, 512], F32)
        gg = cx.sbuf("c_gg", [128, 3, KC], F32)
        acc = PRot(psb[0:6])
        ssq = psb[7]
        for (s, n) in blocks:
            if s == 0 and last:
                continue
            r = NSEQ if s == 0 else b
            cx.tt(gg[:, 0, :], mod_vec(l, 2, r), gnT[:, l, 1, :], ALU.mult)
            cx.ts(gg[:, 1, :], mod_vec(l, 4, r), 1.0, None, ALU.add)
            cx.tt(gg[:, 1, :], gg[:, 1, :], gnT[:, l, 2, :], ALU.mult)
            cx.tt(gg[:, 2, :], mod_vec(l, 5, r), gnT[:, l, 3, :], ALU.mult)
            cx.dma("sp", ab[:, :, 0:n], OT.rearrange("c p t -> p c t")[:, :, s:s + n], "c_ab")
            cx.dma("sp", xT[:, :, 0:n], XT[b].rearrange("c p t -> p c t")[:, :, s:s + n], "c_xT")

            def post_norm(gi):
                rstd_from(ssq[:, 0:n], n, 1.0 / D, rstd, wk)
                for oc in range(KC):
                    t_, _ = tw.next()
                    cx.tt(t_[:, 0:n], yT[:, oc, 0:n], rstd[:, 0:n], ALU.mult)
                    cx.stt(xT[:, oc, 0:n], t_[:, 0:n], gg[:, gi, oc:oc + 1], xT[:, oc, 0:n], ALU.mult, ALU.add)

            def sumsq_step(src, oc):
                sq, _ = sqb.next()
                cx.act(sq[:, 0:n], src, AF.Square)
                cx.group("pe", [cx.mm(ssq[:, 0:n], ones_b, sq[:, 0:n], oc == 0, oc == KC - 1)], [ones_b, sq], [ssq])

            for g in range(4):
                wt, wkey = wb.next()
                wv = wt.rearrange("p (k n) -> p k n", k=KC)
                cx.dma("sp", wv, WOUT[l][g], wkey)
                for j in range(4):
                    oc = g * 4 + j
                    ps = acc.next()
                    cx.matmul_group(ps[:, 0:n], [(wv[:, kc, j * 128:(j + 1) * 128], ab[:, kc, 0:n]) for kc in range(KC)])
                    cx.copy(yT[:, oc, 0:n], ps[:, 0:n], eng="act")
                    sumsq_step(yT[:, oc, 0:n], oc)
            post_norm(0)
            for oc in range(KC):
                sumsq_step(xT[:, oc, 0:n], oc)
            rstd_from(ssq[:, 0:n], n, 1.0 / D, rstd, wk)
            for c in range(KC):
                t_, _ = tw.next()
                cx.tt(t_[:, 0:n], xT[:, c, 0:n], rstd[:, 0:n], ALU.mult)
                cx.act(ab[:, c, 0:n], t_[:, 0:n], AF.Identity, bias=mod_vec(l, 3, r)[:, c:c + 1], scale=gg[:, 1, c:c + 1])
            for hf in range(2):
                for gq in range(NUG // 2):
                    g = hf * (NUG // 2) + gq
                    wt, wkey = wb.next()
                    wv = wt.rearrange("p (k n) -> p k n", k=KC)
                    cx.dma("sp", wv, WUP[l][g], wkey)
                    for j in range(4):
                        hc = gq * 4 + j
                        ps = acc.next()
                        cx.matmul_group(ps[:, 0:n], [(wv[:, kc, j * 128:(j + 1) * 128], ab[:, kc, 0:n]) for kc in range(KC)])
                        r_, _ = rl.next()
                        cx.act(r_[:, 0:n], ps[:, 0:n], AF.Relu)
                        cx.tt(hT[:, hc, 0:n], r_[:, 0:n], r_[:, 0:n], ALU.mult)
                for oc in range(KC):
                    wt, wkey = wb.next()
                    wv = wt[:, 0:HH * 128].rearrange("p (k n) -> p k n", k=HH)
                    cx.dma("sp", wv, WDN[l][hf][oc], wkey)
                    ps = acc.next()
                    cx.matmul_group(ps[:, 0:n], [(wv[:, kc, :], hT[:, kc, 0:n]) for kc in range(HH)])
                    if hf == 0:
                        cx.copy(yT[:, oc, 0:n], ps[:, 0:n], eng="act")
                    else:
                        cx.tt(yT[:, oc, 0:n], ps[:, 0:n], yT[:, oc, 0:n], ALU.add)
                        sumsq_step(yT[:, oc, 0:n], oc)
            post_norm(2)
            cx.dma("pool", XT[b].rearrange("c p t -> p c t")[:, :, s:s + n], xT[:, :, 0:n], "c_xst")
        cx.pop_scope()

    for l in range(L):
        last = l == L - 1
        if l + 1 < L:
            cast_layer(l + 1)
        phase_F(l, ["lat"] if last else ["lat", "ctx"])
        stage(3)
        for b in range(NSEQ):
            phase_A1(l, b, last)
            stage(4)
            phase_A2(l, b, last)
            stage(5)
            phase_H(l, b, ["lat"] if last else ["lat", "ctx"])
            stage(6)
            phase_CD(l, b, last)
            stage(7)

    cx.push_scope()
    xl = Rot(cx, "e_xl", 2, [128, KC, 128], F32)
    os_ = Rot(cx, "e_os", 2, [128, D], F32)
    pr = PRot(psb[0:8])
    for b in range(NSEQ):
        for tt in range(SEQ // 128):
            xt, key = xl.next()
            cx.dma("sp", xt, XT[b].rearrange("c p t -> p c t")[:, :, CTX + tt * 128:CTX + (tt + 1) * 128], key)
            st, sk = os_.next()
            for q in range(KC // 4):
                ps = pr.next()
                fns = [cx.mm(ps[:, j * 128:(j + 1) * 128], xt[:, q * 4 + j, :], ident_f, True, True) for j in range(4)]
                cx.group("pe", fns, [xt, ident_f], [ps])
                cx.copy(st[:, q * 512:(q + 1) * 512], ps, eng=("act" if q % 2 else "dve"))
            cx.dma("pool", out_t[b][tt * 128:(tt + 1) * 128, :].reg((b, tt)), st, sk)
    cx.dead = False
    cx.finish([out_t])
    cx.pop_scope()
    cx.close()
    return cx
GRID_W = 64
ROPE_THETA = 10000.0
HY_CH = 512
_CONST_CACHE = {}


def _axial_rope(n_rows, head_dim):
    t_row = np.repeat(np.arange(n_rows, dtype=np.float32), GRID_W)
    t_col = np.tile(np.arange(GRID_W, dtype=np.float32), n_rows)
    d_axis = head_dim // 2
    inv = (np.float32(ROPE_THETA) ** (-np.arange(0, d_axis, 2, dtype=np.float32) / np.float32(d_axis))).astype(np.float32)
    ang = np.concatenate([t_row[:, None] * inv, t_col[:, None] * inv], axis=-1).astype(np.float32)
    return np.cos(ang).astype(np.float32), np.sin(ang).astype(np.float32)


def _dft_tables(n):
    nfc = n // 128
    N2 = 4 * n
    f = np.arange(n, dtype=np.int64)
    d = np.arange(n, dtype=np.int64)
    ph = ((2 * f[:, None] + 1) * d[None, :]) % N2
    ang = (2.0 * np.pi / N2) * ph.astype(np.float64)
    c = np.cos(ang).astype(np.float32)
    s = np.sin(ang).astype(np.float32)
    cF = np.ascontiguousarray(c.reshape(nfc, 128, nfc, 128).transpose(0, 3, 2, 1))
    sF = np.ascontiguousarray(s.reshape(nfc, 128, nfc, 128).transpose(0, 3, 2, 1))
    cI = np.ascontiguousarray(c.reshape(nfc, 128, n))
    sI = np.ascontiguousarray(s.reshape(nfc, 128, n))
    return cF, sF, cI, sI


def _hyena_consts(n):
    t = np.linspace(0.0, 1.0, n, dtype=np.float32)[:, None]
    w = (np.float32(2.0 * math.pi) * np.arange(n, dtype=np.float32)[:, None] / np.float32(n)).astype(np.float32)
    f = np.linspace(1e-4, 15, 16, dtype=np.float32)[None, :]
    z = np.concatenate([t, np.cos(f * w), -np.sin(f * w)], axis=-1).astype(np.float32)
    min_decay = math.log(1e-2) / 1.5
    max_decay = math.log(1e-2) / 0.3
    deltas = np.linspace(min_decay, max_decay, HY_CH, dtype=np.float32)
    decay = np.exp(-t * np.abs(deltas)).astype(np.float32)
    return np.ascontiguousarray(z.T), np.ascontiguousarray(decay.reshape(n // 128, 128, HY_CH))


def _constants(SEQ):
    if SEQ in _CONST_CACHE:
        return _CONST_CACHE[SEQ]
    c = {}
    c["ident"] = np.eye(128, dtype=np.float32)
    r = np.zeros((128, 128), np.float32)
    r[np.arange(128), np.arange(128) ^ 1] = 1.0
    c["rswap"] = r
    rows = SEQ // GRID_W
    cg, sg = _axial_rope(rows, 128)
    cd, sd = _axial_rope(rows, 64)
    p = np.arange(128)
    sign = np.where(p % 2 == 0, -1.0, 1.0).astype(np.float32)[:, None]
    Cg = cg.T[p // 2]
    Sg = sg.T[p // 2] * sign
    Cd = cd.T[(p % 64) // 2]
    Sd = sd.T[(p % 64) // 2] * sign
    c["rope"] = np.ascontiguousarray(np.stack([Cg, Sg, Cd, Sd]).astype(np.float32))
    for tag, n in (("lat", SEQ), ("ctx", CTX)):
        zT, dec = _hyena_consts(n)
        c[f"zT_{tag}"] = zT
        c[f"decay_{tag}"] = dec
        cF, sF, cI, sI = _dft_tables(n)
        c[f"cmF_{tag}"], c[f"smF_{tag}"], c[f"cmI_{tag}"], c[f"smI_{tag}"] = cF, sF, cI, sI
    _CONST_CACHE[SEQ] = c
    return c


def _pk(v):
    v = np.asarray(v, np.float32)
    lead = v.shape[:-1]
    n = v.shape[-1] // 128
    v = v.reshape(*lead, n, 128)
    return np.ascontiguousarray(np.moveaxis(v, -1, 0))


def prep_shared(inp, cfg):
    L = cfg["L"]
    sh = dict(_constants(cfg["SEQ"]))
    for k in ("w_mod", "w_in", "w_out", "w_up", "w_down"):
        sh[k] = np.ascontiguousarray(inp[k], dtype=np.float32)
    sh["bmodT"] = _pk(inp["b_mod"])
    sh["gnT"] = _pk(inp["g_norm"])
    sh["vec128"] = np.ascontiguousarray(np.stack(
        [np.asarray(inp[k], np.float32) for k in ("diff_subln", "gqa_q_norm", "gqa_k_norm", "gqa_out_norm")], axis=-1
    ).transpose(1, 0, 2))
    lam = np.asarray(inp["diff_lam"], np.float32).reshape(L, 256)
    sh["lamp"] = np.ascontiguousarray(np.broadcast_to(lam[None], (128, L, 256)))
    cw = _pk(np.asarray(inp["hy_conv_w"], np.float32))
    sh["convw"] = np.ascontiguousarray(cw.transpose(0, 1, 3, 2))
    sh["convb"] = _pk(inp["hy_conv_b"])
    sh["hyon"] = _pk(inp["hy_out_norm"])
    sh["hybias"] = np.ascontiguousarray(np.asarray(inp["hy_bias"], np.float32)[None])
    sh["hw1"] = np.ascontiguousarray(inp["hy_w1"], dtype=np.float32)
    sh["hw2"] = np.ascontiguousarray(inp["hy_w2"], dtype=np.float32)
    sh["hw3"] = np.ascontiguousarray(inp["hy_w3"], dtype=np.float32)
    sh["hwout"] = np.ascontiguousarray(inp["hy_wout"], dtype=np.float32)
    hb = np.stack([np.asarray(inp[k], np.float32) for k in ("hy_b1", "hy_b2", "hy_b3")], axis=-1)
    sh["hbT"] = np.ascontiguousarray(hb.transpose(1, 0, 2))
    sh["hfT"] = np.ascontiguousarray(np.asarray(inp["hy_freq"], np.float32).transpose(2, 0, 1))
    return sh


def prep_core(inp, cfg, core, sh):
    NSEQ = cfg["NSEQ"]
    b0 = core * NSEQ
    m = dict(sh)
    m["x"] = np.ascontiguousarray(inp["x"][b0:b0 + NSEQ], dtype=np.float32)
    m["ctx"] = np.ascontiguousarray(inp["ctx"][b0:b0 + NSEQ], dtype=np.float32)
    rows = np.concatenate([np.asarray(inp["c"], np.float32)[b0:b0 + NSEQ], np.asarray(inp["c_ctx"], np.float32)[None]], axis=0)
    m["scT"] = np.ascontiguousarray(rows.reshape(NSEQ + 1, KC, 128).transpose(2, 1, 0))
    return m


def run_cfg(inp, cfg, n_cores):
    nc = bass.Bass("TRN2", target_bir_lowering=False)
    build_program(nc, cfg)
    sh = prep_shared(inp, cfg)
    in_maps = [prep_core(inp, cfg, c, sh) for c in range(n_cores)]
    res = run_bass_kernel_spmd(nc, in_maps, core_ids=list(range(n_cores)))
    if cfg.get("dbgout"):
        return res.results
    return np.concatenate([np.asarray(r["out"], dtype=np.float32) for r in res.results], axis=0)


def kernel(**inputs):
    cfg = dict(L=4, SEQ=2048, NSEQ=2, DFF=8192)
    return run_cfg(inputs, cfg, 8)
```
